# Optimizing a Trainium2 kernel written in Bass

```python
import jax, jax.numpy as jnp
from jax import lax
import numpy as np

D_MODEL = 1024
BATCH = 1
SEQ = 16384
DEPTH = 4

GRID_W = 64
CTX_LEN = 256
N_MIXERS = 2
N_A = (DEPTH + N_MIXERS - 1) // N_MIXERS
N_B = DEPTH // N_MIXERS
LRU_WIDTH = D_MODEL
LRU_HEADS = 4
LRU_BLOCK = LRU_WIDTH // LRU_HEADS
LRU_C = 8.0
CONV_A_WIDTH = 4
CONV_A_PAD = 2
CONV_B_WIDTH = 3
FFN_CONV_WIDTH = 3
D_FF = 2816
N_MOD = 6
EPS = 1e-6

kernel_name = "hybrid_rglru_shortconv_dit_prefix"


def rms_norm(x, g):
    xf = x.astype(jnp.float32)
    y = xf * lax.rsqrt(jnp.mean(xf * xf, axis=-1, keepdims=True) + EPS) * g.astype(jnp.float32)
    return y.astype(x.dtype)


def modulate(h, shift, scale):
    return h * (1.0 + scale) + shift


def adaln(cvec, w, b):
    return jnp.split(jax.nn.silu(cvec) @ w + b, N_MOD, axis=-1)


def dwconv(x, w, b, pad_lo, axis):
    k_width = w.shape[0]
    n = x.shape[axis]
    pads = [(0, 0)] * x.ndim
    pads[axis] = (pad_lo, k_width - 1 - pad_lo)
    xp = jnp.pad(x, pads)
    out = b + w[0] * lax.slice_in_dim(xp, 0, n, axis=axis)
    for k in range(1, k_width):
        out = out + w[k] * lax.slice_in_dim(xp, k, k + n, axis=axis)
    return out


def seq_conv(x, w, b, pad_lo):
    return dwconv(x, w, b, pad_lo, axis=1)


def grid_conv(x, w, b, pad_lo, rows, grid_axis):
    bsz, n, ch = x.shape
    g = x.reshape(bsz, rows, GRID_W, ch)
    return dwconv(g, w, b, pad_lo, axis=grid_axis).reshape(bsz, n, ch)


def _lin_combine(e1, e2):
    a1, b1 = e1
    a2, b2 = e2
    return a1 * a2, a2 * b1 + b2


def linear_scan(a, b, h0, reverse):
    a_cum, b_cum = lax.associative_scan(_lin_combine, (a, b), reverse=reverse, axis=1)
    if h0 is None:
        return b_cum
    return b_cum + a_cum * h0[:, None, :]


def rglru_coeffs(u, w_r, b_r, w_i, b_i, lam):
    bsz, n, width = u.shape
    ub = u.reshape(bsz, n, LRU_HEADS, LRU_BLOCK)
    r = jax.nn.sigmoid(jnp.einsum('bnhi,hij->bnhj', ub, w_r) + b_r).reshape(bsz, n, width).astype(jnp.float32)
    i = jax.nn.sigmoid(jnp.einsum('bnhi,hij->bnhj', ub, w_i) + b_i).reshape(bsz, n, width).astype(jnp.float32)
    log_a = LRU_C * r * jax.nn.log_sigmoid(lam.astype(jnp.float32))
    a = jnp.exp(log_a)
    b = jnp.sqrt(-jnp.expm1(2.0 * log_a)) * (i * u.astype(jnp.float32))
    return a, b


def rglru_direction(ul, uc, w_r, b_r, w_i, b_i, lam, reverse):
    a_c, b_c = rglru_coeffs(uc, w_r, b_r, w_i, b_i, lam)
    h_c = linear_scan(a_c, b_c, None, reverse)
    h0 = h_c[:, 0] if reverse else h_c[:, -1]
    a_l, b_l = rglru_coeffs(ul, w_r, b_r, w_i, b_i, lam)
    h_l = linear_scan(a_l, b_l, h0, reverse)
    return h_l, h_c


def rglru_mixer(hl, hc, w_x, w_gate, conv_w, conv_b, w_r, b_r, w_i, b_i, lam, w_out, ctx_out):
    ul = seq_conv(hl @ w_x, conv_w, conv_b, CONV_A_PAD)
    uc = seq_conv(hc @ w_x, conv_w, conv_b, CONV_A_PAD)
    hl_f, hc_f = rglru_direction(ul, uc, w_r[0], b_r[0], w_i[0], b_i[0], lam[0], False)
    hl_b, hc_b = rglru_direction(ul, uc, w_r[1], b_r[1], w_i[1], b_i[1], lam[1], True)
    yl = (jax.nn.gelu(hl @ w_gate) * (hl_f + hl_b).astype(hl.dtype)) @ w_out
    yc = None
    if ctx_out:
        yc = (jax.nn.gelu(hc @ w_gate) * (hc_f + hc_b).astype(hc.dtype)) @ w_out
    return yl, yc


def shortconv_mixer(h, w_in, conv_w, conv_b, w_out, conv_fn):
    g_b, g_c, v = jnp.split(h @ w_in, 3, axis=-1)
    return (g_b * conv_fn(g_c * v, conv_w, conv_b)) @ w_out


def conv_ffn(h, w_up, conv_w, conv_b, w_down, conv_fn):
    g, v = jnp.split(h @ w_up, 2, axis=-1)
    return (jax.nn.silu(conv_fn(g, conv_w, conv_b)) * v) @ w_down


def setup_inputs(seed: int = 0) -> dict:
    key = jax.random.key(seed)
    ks = jax.random.split(key, 32)
    f32 = jnp.float32
    D = D_MODEL

    def nrm(k, shape, scale):
        return jax.random.normal(k, shape, f32) * scale

    lam_a = jax.random.uniform(ks[14], (N_A, 2, LRU_WIDTH), f32, 0.9, 0.999)
    s = lam_a ** (1.0 / LRU_C)
    a_lambda = jnp.log(s) - jnp.log1p(-s)
    return {
        "x": nrm(ks[0], (BATCH, SEQ, D), 1.0),
        "c": nrm(ks[1], (BATCH, D), 1.0),
        "ctx": nrm(ks[2], (BATCH, CTX_LEN, D), 1.0),
        "c_ctx": nrm(ks[3], (D,), 1.0),
        "ada_w": nrm(ks[4], (DEPTH, D, N_MOD * D), D ** -0.5),
        "ada_b": nrm(ks[5], (DEPTH, N_MOD * D), 0.02),
        "norm_mix_g": 1.0 + nrm(ks[6], (DEPTH, D), 0.02),
        "norm_ffn_g": 1.0 + nrm(ks[7], (DEPTH, D), 0.02),
        "a_w_x": nrm(ks[8], (N_A, D, LRU_WIDTH), D ** -0.5),
        "a_w_gate": nrm(ks[9], (N_A, D, LRU_WIDTH), D ** -0.5),
        "a_conv_w": nrm(ks[10], (N_A, CONV_A_WIDTH, LRU_WIDTH), CONV_A_WIDTH ** -0.5),
        "a_conv_b": nrm(ks[11], (N_A, LRU_WIDTH), 0.01),
        "a_w_r": nrm(ks[12], (N_A, 2, LRU_HEADS, LRU_BLOCK, LRU_BLOCK), LRU_BLOCK ** -0.5),
        "a_b_r": nrm(ks[13], (N_A, 2, LRU_HEADS, LRU_BLOCK), 0.01),
        "a_w_i": nrm(ks[15], (N_A, 2, LRU_HEADS, LRU_BLOCK, LRU_BLOCK), LRU_BLOCK ** -0.5),
        "a_b_i": nrm(ks[16], (N_A, 2, LRU_HEADS, LRU_BLOCK), 0.01),
        "a_lambda": a_lambda,
        "a_w_out": nrm(ks[17], (N_A, LRU_WIDTH, D), LRU_WIDTH ** -0.5),
        "b_w_in": nrm(ks[18], (N_B, D, 3 * D), D ** -0.5),
        "b_conv_w": nrm(ks[19], (N_B, CONV_B_WIDTH, D), CONV_B_WIDTH ** -0.5),
        "b_conv_b": nrm(ks[20], (N_B, D), 0.01),
        "b_w_out": nrm(ks[21], (N_B, D, D), D ** -0.5),
        "f_w_up": nrm(ks[22], (DEPTH, D, 2 * D_FF), D ** -0.5),
        "f_conv_w": nrm(ks[23], (DEPTH, FFN_CONV_WIDTH, D_FF), FFN_CONV_WIDTH ** -0.5),
        "f_conv_b": nrm(ks[24], (DEPTH, D_FF), 0.01),
        "f_w_down": nrm(ks[25], (DEPTH, D_FF, D), D_FF ** -0.5),
        "final_g": 1.0 + nrm(ks[26], (D,), 0.02),
    }


def reference(x, c, ctx, c_ctx, ada_w, ada_b, norm_mix_g, norm_ffn_g,
              a_w_x, a_w_gate, a_conv_w, a_conv_b, a_w_r, a_b_r, a_w_i, a_b_i, a_lambda, a_w_out,
              b_w_in, b_conv_w, b_conv_b, b_w_out,
              f_w_up, f_conv_w, f_conv_b, f_w_down, final_g):
    rows = x.shape[1] // GRID_W

    def latent_vconv(t, w, b):
        return grid_conv(t, w, b, 1, rows, 1)

    def latent_hconv(t, w, b):
        return grid_conv(t, w, b, 1, rows, 2)

    def ctx_conv(t, w, b):
        return seq_conv(t, w, b, 1)

    h_ctx = ctx
    for l in range(DEPTH):
        j = l // N_MIXERS
        ctx_needed = any((k % N_MIXERS) == 0 for k in range(l + 1, DEPTH))
        sm, cm, gm, sf, cf, gf = [m[:, None, :] for m in adaln(c, ada_w[l], ada_b[l])]
        smc, cmc, gmc, sfc, cfc, gfc = adaln(c_ctx, ada_w[l], ada_b[l])

        hl = modulate(rms_norm(x, norm_mix_g[l]), sm, cm)
        yc = None
        if l % N_MIXERS == 0:
            hc = modulate(rms_norm(h_ctx, norm_mix_g[l]), smc, cmc)
            yl, yc = rglru_mixer(hl, hc, a_w_x[j], a_w_gate[j], a_conv_w[j], a_conv_b[j],
                                 a_w_r[j], a_b_r[j], a_w_i[j], a_b_i[j], a_lambda[j], a_w_out[j],
                                 ctx_needed)
        else:
            yl = shortconv_mixer(hl, b_w_in[j], b_conv_w[j], b_conv_b[j], b_w_out[j], latent_vconv)
            if ctx_needed:
                hc = modulate(rms_norm(h_ctx, norm_mix_g[l]), smc, cmc)
                yc = shortconv_mixer(hc, b_w_in[j], b_conv_w[j], b_conv_b[j], b_w_out[j], ctx_conv)
        x = x + (gm * yl).astype(x.dtype)
        if ctx_needed:
            h_ctx = h_ctx + (gmc * yc).astype(h_ctx.dtype)

        hl = modulate(rms_norm(x, norm_ffn_g[l]), sf, cf)
        x = x + (gf * conv_ffn(hl, f_w_up[l], f_conv_w[l], f_conv_b[l], f_w_down[l], latent_hconv)).astype(x.dtype)
        if ctx_needed:
            hc = modulate(rms_norm(h_ctx, norm_ffn_g[l]), sfc, cfc)
            h_ctx = h_ctx + (gfc * conv_ffn(hc, f_w_up[l], f_conv_w[l], f_conv_b[l], f_w_down[l], ctx_conv)).astype(h_ctx.dtype)

    return rms_norm(x, final_g)
```

```python
from contextlib import ExitStack
import numpy as np
import concourse.bass as bass
import concourse.mybir as mybir
from concourse.bass_utils import run_bass_kernel_spmd

F32 = mybir.dt.float32
F32R = mybir.dt.float32r
AF = mybir.ActivationFunctionType
ALU = mybir.AluOpType

D = 1024
SEQ = 16384
NCORE = 8
OWN = SEQ // NCORE
ROWW = 64
HALO_ROWS = 3
MARG = 2
NROWS = OWN // ROWW + 2 * HALO_ROWS
XW = NROWS * ROWW + 2 * MARG
CTXN = 256
DFF = 2816
NJ = DFF // 128
EPS = 1e-6
LRU_C = 8.0
DEPTH = 4
NSLOT = 5
NDMA = 12
NFU = 14
NRU = 12
NUNIT = NFU + NRU
VMW = 704
GELU_K = 0.7978845608028654
GELU_C = 0.044715

FFN_GROUPS = [(0, 6), (6, 12), (12, 17), (17, 22)]


def col(row):
    return MARG + ROWW * row


def R(ap):
    return ap.bitcast(F32R)


class Buf:
    __slots__ = ("w", "r", "name")

    def __init__(self, name=""):
        self.w = None
        self.r = {}
        self.name = name


class Eng:
    def __init__(self, name):
        self.name = name
        self.ops = []
        self.count = 0
        self.known = {}


class Pool:
    def __init__(self, items, name):
        self.free = list(items)
        self.name = name

    def get(self):
        assert self.free, f"pool {self.name} exhausted"
        return self.free.pop(0)

    def put(self, it):
        self.free.append(it)


class Prog:
    def __init__(self):
        self.engs = {n: Eng(n) for n in ("pe", "act", "dve", "pool", "sp")}
        self.dma_cnt = [0] * NDMA
        self.dma_rr = 0

    def _waits(self, eng, reads, writes):
        need = {}
        for b in reads:
            if b.w is not None:
                need[b.w[0]] = max(need.get(b.w[0], 0), b.w[1])
        for b in writes:
            if b.w is not None:
                need[b.w[0]] = max(need.get(b.w[0], 0), b.w[1])
            for k, v in b.r.items():
                need[k] = max(need.get(k, 0), v)
        waits = []
        for k, v in need.items():
            if eng.known.get(k, 0) < v:
                waits.append((k, v))
                eng.known[k] = v
        return waits

    def _commit(self, tok, reads, writes):
        for b in writes:
            b.w = tok
            b.r = {}
        for b in reads:
            b.r[tok[0]] = max(b.r.get(tok[0], 0), tok[1])

    def op(self, engname, fn, reads=(), writes=()):
        eng = self.engs[engname]
        waits = self._waits(eng, reads, writes)
        eng.count += 1
        tok = (engname, eng.count)
        eng.ops.append((waits, fn, tok, 1))
        self._commit(tok, reads, writes)
        return tok

    def dma(self, fn, reads=(), writes=(), engname="sp"):
        eng = self.engs[engname]
        j = self.dma_rr
        self.dma_rr = (j + 1) % NDMA
        key = "dma%d" % j
        waits = self._waits(eng, reads, writes)
        prev = self.dma_cnt[j]
        if prev and eng.known.get(key, 0) < prev:
            waits.append((key, prev))
            eng.known[key] = prev
        self.dma_cnt[j] += 16
        tok = (key, self.dma_cnt[j])
        eng.ops.append((waits, fn, tok, 16))
        self._commit(tok, reads, writes)
        return tok


def colvec(v, nch):
    return np.ascontiguousarray(np.asarray(v, np.float32).reshape(nch, 128).T)


class PrmLayout:
    def __init__(self):
        self.off = {}
        self.n = 0

    def add(self, name, w):
        self.off[name] = self.n
        self.n += w


def prm_layout():
    L = PrmLayout()
    for l in range(DEPTH):
        L.add(("nmg", l), 8)
        L.add(("nfg", l), 8)
        L.add(("adab", l), 48)
        for t in range(3):
            L.add(("fcw", l, t), NJ)
        L.add(("fcb", l), NJ)
    L.add("fing", 8)
    for j in range(2):
        for t in range(4):
            L.add(("acw", j, t), 8)
        L.add(("acb", j), 8)
        for d in range(2):
            L.add(("abr", j, d), 8)
            L.add(("abi", j, d), 8)
            L.add(("alam", j, d), 8)
        for t in range(3):
            L.add(("bcw", j, t), 8)
        L.add(("bcb", j), 8)
    L.add("cvec", 16)
    L.add("mf", 8)
    L.add("mb", 8)
    return L


PL = prm_layout()


def drv_layout():
    L = PrmLayout()
    for l in range(DEPTH):
        for s in range(2):
            for nm in ("gsm", "shm", "gm", "gsf", "shf", "gf"):
                L.add((nm, l, s), 8)
    for j in range(2):
        for d in range(2):
            L.add(("hcl", j, d), 8)
            L.add(("hbr", j, d), 8)
            L.add(("hbi", j, d), 8)
    return L


DL = drv_layout()


def blockw(W):
    K, M = W.shape
    return np.ascontiguousarray(
        W.reshape(K // 128, 128, M // 128, 128).transpose(2, 1, 0, 3)).reshape(M // 128, 128, (K // 128) * 128)


def pack_host(inp):
    f = lambda a: np.asarray(a, np.float32)
    prm_common = np.zeros((128, PL.n), np.float32)

    def put(name, arr):
        o = PL.off[name]
        prm_common[:, o:o + arr.shape[1]] = arr

    for l in range(DEPTH):
        put(("nmg", l), colvec(f(inp["norm_mix_g"])[l], 8))
        put(("nfg", l), colvec(f(inp["norm_ffn_g"])[l], 8))
        put(("adab", l), colvec(f(inp["ada_b"])[l], 48))
        for t in range(3):
            put(("fcw", l, t), colvec(f(inp["f_conv_w"])[l, t], NJ))
        put(("fcb", l), colvec(f(inp["f_conv_b"])[l], NJ))
    put("fing", colvec(f(inp["final_g"]), 8))
    for j in range(2):
        for t in range(4):
            put(("acw", j, t), colvec(f(inp["a_conv_w"])[j, t], 8))
        put(("acb", j), colvec(f(inp["a_conv_b"])[j], 8))
        for d in range(2):
            put(("abr", j, d), colvec(f(inp["a_b_r"])[j, d].reshape(-1), 8))
            put(("abi", j, d), colvec(f(inp["a_b_i"])[j, d].reshape(-1), 8))
            put(("alam", j, d), colvec(f(inp["a_lambda"])[j, d], 8))
        for t in range(3):
            put(("bcw", j, t), colvec(f(inp["b_conv_w"])[j, t], 8))
        put(("bcb", j), colvec(f(inp["b_conv_b"])[j], 8))
    cv = np.concatenate([colvec(f(inp["c"]).reshape(-1), 8), colvec(f(inp["c_ctx"]).reshape(-1), 8)], axis=1)
    put("cvec", cv)

    x = f(inp["x"])[0]
    per_core = []
    for k in range(NCORE):
        t0 = k * OWN - HALO_ROWS * ROWW - MARG
        xt = np.zeros((XW, D), np.float32)
        lo, hi = max(t0, 0), min(t0 + XW, SEQ)
        xt[lo - t0:hi - t0] = x[lo:hi]
        vm = np.zeros((XW,), np.float32)
        vm[lo - t0:hi - t0] = 1.0
        prm = prm_common.copy()
        mf = np.array([1.0 if j < k else 0.0 for j in range(NCORE)], np.float32)
        mb = np.array([1.0 if j > k else 0.0 for j in range(NCORE)], np.float32)
        prm[:, PL.off["mf"]:PL.off["mf"] + 8] = mf[None, :]
        prm[:, PL.off["mb"]:PL.off["mb"] + 8] = mb[None, :]
        per_core.append({
            "xT": np.ascontiguousarray(xt.T.reshape(8, 128, XW)),
            "vml": np.ascontiguousarray(np.broadcast_to(vm[None, :VMW], (128, VMW))),
            "vmr": np.ascontiguousarray(np.broadcast_to(vm[None, XW - VMW:], (128, VMW))),
            "prm": prm,
        })
    shared = {
        "ctxT": np.ascontiguousarray(f(inp["ctx"])[0].T.reshape(8, 128, CTXN)),
        "ada_w": np.ascontiguousarray(f(inp["ada_w"])),
    }
    for j in range(2):
        shared["awx%d" % j] = blockw(f(inp["a_w_x"])[j])
        shared["awg%d" % j] = blockw(f(inp["a_w_gate"])[j])
        shared["awo%d" % j] = blockw(f(inp["a_w_out"])[j])
        shared["bwi%d" % j] = blockw(f(inp["b_w_in"])[j])
        shared["bwo%d" % j] = blockw(f(inp["b_w_out"])[j])
        g = np.zeros((2, 2, 128, 4 * 2 * 2 * 128), np.float32)
        for d in range(2):
            for gi, nm in enumerate(("a_w_r", "a_w_i")):
                w = f(inp[nm])[j, d]
                wb = w.reshape(4, 2, 128, 2, 128).transpose(2, 0, 3, 1, 4)
                g[d, gi] = wb.reshape(128, -1)
        shared["agt%d" % j] = g
    for l in range(DEPTH):
        shared["fwu%d" % l] = blockw(f(inp["f_w_up"])[l])
        shared["fwd%d" % l] = blockw(f(inp["f_w_down"])[l])
    return per_core, shared


def split_rows(lo, hi, maxr):
    n = hi - lo
    nt = -(-n // maxr)
    base, rem = divmod(n, nt)
    out, r = [], lo
    for i in range(nt):
        w = base + (1 if i < rem else 0)
        out.append((r, r + w))
        r += w
    return out


def layer_rows(l):
    return (l, NROWS - l)


class Builder:
    def __init__(self, steps, outs):
        self.steps = steps
        self.outs = outs
        self.nc = bass.Bass("TRN2", target_bir_lowering=False)
        self.nc.dge_precook = False
        self.P = Prog()
        self.dram = {}
        self.stack = ExitStack()

    def din(self, name, shape, dt=F32):
        if name not in self.dram:
            self.dram[name] = self.nc.dram_tensor(name, list(shape), dt, kind="ExternalInput").ap()
        return self.dram[name]

    def dout(self, name, shape):
        self.dram[name] = self.nc.dram_tensor(name, list(shape), F32, kind="ExternalOutput").ap()
        return self.dram[name]

    def sb(self, name, shape, dt=F32):
        return self.stack.enter_context(self.nc.sbuf_tensor(name, list(shape), dt))

    def A(self, fn, r=(), w=()):
        return self.P.op("act", fn, r, w)

    def V(self, fn, r=(), w=()):
        return self.P.op("dve", fn, r, w)

    def G(self, fn, r=(), w=()):
        return self.P.op("pool", fn, r, w)

    def T(self, fn, r=(), w=()):
        return self.P.op("pe", fn, r, w)

    def DMA(self, fn, r=(), w=()):
        return self.P.dma(fn, r, w)

    def setup(self):
        nc = self.nc
        self.X = self.sb("X", [128, 8, XW])
        self.Xb = [[Buf("x%d_%d" % (c, i)) for i in range(XW // 64 + 1)] for c in range(8)]
        if "ctx" in self.outs["ins"]:
            self.HC = self.sb("HCTX", [128, 8, CTXN])
        self.HCb = [Buf("hctx%d" % c) for c in range(8)]
        self.HLt = self.sb("HL", [128, 16, 512])
        self.HLb = [Buf("hl%d" % i) for i in range(16)]
        self.SCR = self.sb("SCR", [128, NFU, 512])
        self.RSCR = self.sb("RSCR", [128, NRU, 512])
        self.units = Pool([(self.SCR[:, i, :], Buf("u%d" % i)) for i in range(NFU)], "units")
        self.runits = Pool([(self.RSCR[:, i, :], Buf("r%d" % i)) for i in range(NRU)], "runits")
        self.WS = self.sb("WS", [128, NSLOT, 1024], F32R)
        self.slots = Pool([(self.WS[:, i, :], Buf("ws%d" % i)) for i in range(NSLOT)], "slots")
        self.PS = [self.stack.enter_context(nc.psum_tensor("ps%d" % i, [128, 512], F32)) for i in range(8)]
        self.psums = Pool([(self.PS[i][:, :], Buf("ps%d" % i)) for i in range(8)], "psum")
        self.PRM = self.sb("PRM", [128, PL.n])
        self.PRMb = Buf("prm")
        self.DRV = self.sb("DRV", [128, DL.n])
        self.DRVb = Buf("drv")
        self.VML = self.sb("VML", [128, VMW])
        self.VMR = self.sb("VMR", [128, VMW])
        self.VMb = Buf("vm")
        self.ONES = self.sb("ONES", [128, 128])
        self.ONESb = Buf("ones")
        self.RSTD = self.sb("RSTD", [128, 2, 512])
        self.RSTDb = [Buf("rstd0"), Buf("rstd1")]
        self.SM = self.sb("SM", [128, 192])
        self.SMb = Buf("sm")
        self.HCS = self.sb("HCS", [128, 16])
        self.HCSb = Buf("hcs")
        self.AGG = self.sb("AGG", [128, 48 + 16 * 8])
        self.AGGb = Buf("agg")
        self.CAR = self.sb("CAR", [128, 8 + 8 * 8])
        self.CARb = Buf("car")
        self.GATH = self.sb("GATH", [128, NCORE * 32])
        self.GATHb = Buf("gath")
        self.sems = {}
        for n in ("pe", "act", "dve", "pool", "sp"):
            self.sems[n] = self.stack.enter_context(nc.semaphore("s_" + n))
        for j in range(NDMA):
            self.sems["dma%d" % j] = self.stack.enter_context(nc.semaphore("s_dma%d" % j))

    def prm(self, name, i=0, w=1):
        o = PL.off[name] + i
        return self.PRM[:, o:o + w]

    def drv(self, name, i=0, w=1):
        o = DL.off[name] + i
        return self.DRV[:, o:o + w]

    def xbufs(self, c, a, b):
        return self.Xb[c][a // 64:(b - 1) // 64 + 1]

    def vm(self, a, b):
        if b <= VMW:
            return self.VML[:, a:b]
        assert a >= XW - VMW, (a, b)
        return self.VMR[:, a - (XW - VMW):b - (XW - VMW)]

    def wload(self, src_ap, w):
        slot, sbuf = self.slots.get()
        self.DMA(lambda e, slot=slot, src_ap=src_ap, w=w: e.dma_start(out=slot[:, 0:w], in_=src_ap), (), (sbuf,))
        return slot, sbuf

    def mm(self, ps, psb, pieces, reads, n):
        def fn(e, ps=ps, pieces=pieces, n=n):
            ins = None
            for i, (lt, rh) in enumerate(pieces):
                ins = e.matmul(ps[:, 0:n], lhsT=lt, rhs=rh, start=(i == 0), stop=(i == len(pieces) - 1))
            return ins
        self.T(fn, reads, (psb,))

    def norm_mod(self, srcs, hl0, gs, sh, rslot):
        n = srcs[0][0].shape[1]
        ps, psb = self.psums.get()
        sq = [self.runits.get() for _ in range(2)]
        for c in range(8):
            ap, bufs = srcs[c]
            s, sbf = sq[c % 2]
            self.A(lambda e, s=s, ap=ap, n=n: e.activation(out=R(s[:, 0:n]), in_=ap, func=AF.Square), bufs, (sbf,))
            self.T(lambda e, ps=ps, s=s, n=n, c=c: e.matmul(ps[:, 0:n], lhsT=R(self.ONES[:, :]), rhs=R(s[:, 0:n]),
                                                             start=(c == 0), stop=(c == 7)),
                   (sbf, self.ONESb), (psb,))
        for u in sq:
            self.runits.put(u)
        rs = self.RSTD[:, rslot, 0:n]
        rsb = self.RSTDb[rslot]
        self.A(lambda e, rs=rs, ps=ps, n=n: e.activation(out=rs, in_=ps[:, 0:n], func=AF.Sqrt, bias=self.epsc, scale=1.0),
               (psb, self.ONESb), (rsb,))
        self.psums.put((ps, psb))
        self.V(lambda e, rs=rs: e.reciprocal(out=rs, in_=rs), (rsb,), (rsb,))
        xr = [self.units.get() for _ in range(2)]
        for c in range(8):
            ap, bufs = srcs[c]
            t, tb = xr[c % 2]
            self.V(lambda e, t=t, ap=ap, rs=rs, n=n: e.tensor_tensor(out=t[:, 0:n], in0=ap, in1=rs, op=ALU.mult),
                   list(bufs) + [rsb], (tb,))
            hl = self.HLt[:, hl0 + c, 0:n]
            self.A(lambda e, hl=hl, t=t, n=n, c=c: e.activation(out=R(hl), in_=t[:, 0:n], func=AF.Identity,
                                                                 scale=gs[:, c:c + 1], bias=sh[:, c:c + 1]),
                   (tb, self.DRVb), (self.HLb[hl0 + c],))
        for u in xr:
            self.units.put(u)

    def consts(self):
        nc = self.nc
        ada = self.din("ada_w", [DEPTH, D, 6 * D], F32R)
        S = self.sb("SILU", [128, 8, 128])
        Sb = Buf("silu")
        MODROW = self.sb("MODROW", [2, 1024])
        MODROWb = Buf("modrow")
        IDN = self.sb("IDN", [2, 2])
        IDNb = Buf("idn")
        zu, zub = self.units.get()
        self.V(lambda e: e.memset(zu[:, :], 0.0), (), (zub,))
        z3 = zu[:, :].rearrange("p (k c) -> p k c", c=128)
        for hh in range(2):
            self.V(lambda e, hh=hh: e.tensor_copy(out=R(S[:, 4 * hh:4 * hh + 4, :]), in_=z3), (zub,), (Sb,))
        self.units.put((zu, zub))
        self.V(lambda e: e.memset(IDN[:, :], 0.0), (), (IDNb,))
        u, ub = self.units.get()
        cv = self.prm("cvec", 0, 16)
        self.A(lambda e: e.activation(out=u[:, 0:16], in_=cv, func=AF.Tanh, scale=0.5), (self.PRMb,), (ub,))
        self.V(lambda e: e.scalar_tensor_tensor(out=u[:, 0:16], in0=u[:, 0:16], scalar=1.0, in1=cv, op0=ALU.add, op1=ALU.mult),
               (ub, self.PRMb), (ub,))
        for v in range(2):
            self.V(lambda e, v=v: e.tensor_scalar(out=R(S[:, :, v]), in0=u[:, 8 * v:8 * v + 8], scalar1=0.5, scalar2=None, op0=ALU.mult),
                   (ub, Sb), (Sb,))
        self.V(lambda e: e.memset(IDN[0:1, 0:1], 1.0), (IDNb,), (IDNb,))
        self.DMA(lambda e: e.dma_start(out=IDN[1:2, 1:2], in_=IDN[0:1, 0:1]), (IDNb,), (IDNb,))
        self.units.put((u, ub))
        for l in range(DEPTH):
            pt, ptb = self.psums.get()
            for nb in range(6):
                p0, p0b = self.psums.get()
                p1, p1b = self.psums.get()
                for k in range(8):
                    slot, sbuf = self.wload(ada[l, k * 128:(k + 1) * 128, nb * 1024:(nb + 1) * 1024], 1024)
                    self.T(lambda e, p0=p0, slot=slot, k=k: e.matmul(p0[:, :], lhsT=R(S[:, k, :]), rhs=slot[:, 0:512],
                                                                      start=(k == 0), stop=(k == 7)), (sbuf, Sb), (p0b,))
                    self.T(lambda e, p1=p1, slot=slot, k=k: e.matmul(p1[:, :], lhsT=R(S[:, k, :]), rhs=slot[:, 512:1024],
                                                                      start=(k == 0), stop=(k == 7)), (sbuf, Sb), (p1b,))
                    self.slots.put((slot, sbuf))
                self.A(lambda e, p0=p0: e.activation(out=MODROW[0:2, 0:512], in_=p0[0:2, :], func=AF.Copy),
                       (p0b, MODROWb), (MODROWb,))
                self.A(lambda e, p1=p1: e.activation(out=MODROW[0:2, 512:1024], in_=p1[0:2, :], func=AF.Copy),
                       (p1b, MODROWb), (MODROWb,))
                self.psums.put((p0, p0b))
                self.psums.put((p1, p1b))

                def tr(e, pt=pt, nb=nb):
                    ins = None
                    for i in range(8):
                        g = nb * 8 + i
                        ins = e.matmul(pt[:, 2 * g:2 * g + 2], lhsT=MODROW[0:2, i * 128:(i + 1) * 128], rhs=IDN[0:2, 0:2],
                                       start=True, stop=True)
                    return ins
                self.T(tr, (MODROWb, IDNb, ptb), (ptb,))
            mt, mtb = self.units.get()
            mt3 = mt[:, 0:96].rearrange("p (i v) -> p i v", v=2)
            pt3 = pt[:, 0:96].rearrange("p (i v) -> p i v", v=2)
            for v in range(2):
                self.V(lambda e, v=v, mt3=mt3, pt3=pt3, l=l: e.tensor_tensor(out=mt3[:, :, v], in0=pt3[:, :, v],
                                                                            in1=self.prm(("adab", l), 0, 48), op=ALU.add),
                       (ptb, self.PRMb, mtb), (mtb,))
            self.psums.put((pt, ptb))
            for s in range(2):
                md = lambda i, s=s, mt3=mt3: mt3[:, i * 8:(i + 1) * 8, s]
                one = [
                    (("gsm", l, s), md(1), self.prm(("nmg", l), 0, 8)),
                    (("gsf", l, s), md(4), self.prm(("nfg", l), 0, 8)),
                ]
                for nm, m_ap, g_ap in one:
                    self.V(lambda e, nm=nm, m_ap=m_ap, g_ap=g_ap: e.scalar_tensor_tensor(
                        out=self.drv(nm, 0, 8), in0=m_ap, scalar=1.0, in1=g_ap, op0=ALU.add, op1=ALU.mult),
                        (mtb, self.PRMb, self.DRVb), (self.DRVb,))
                for nm, i in ((("shm", l, s), 0), (("gm", l, s), 2), (("shf", l, s), 3), (("gf", l, s), 5)):
                    self.V(lambda e, nm=nm, i=i, md=md: e.tensor_copy(out=self.drv(nm, 0, 8), in_=md(i)),
                           (mtb, self.DRVb), (self.DRVb,))
            self.units.put((mt, mtb))
        u, ub = self.units.get()
        for j in range(2):
            for d in range(2):
                lam = self.prm(("alam", j, d), 0, 8)
                self.A(lambda e, lam=lam: e.activation(out=u[:, 0:8], in_=lam, func=AF.Exp, scale=-1.0), (self.PRMb, ub), (ub,))
                self.A(lambda e: e.activation(out=u[:, 0:8], in_=u[:, 0:8], func=AF.Ln, bias=1.0, scale=1.0), (ub,), (ub,))
                self.V(lambda e, j=j, d=d: e.tensor_scalar(out=self.drv(("hcl", j, d), 0, 8), in0=u[:, 0:8],
                                                             scalar1=-0.5 * LRU_C, scalar2=None, op0=ALU.mult),
                       (ub, self.DRVb), (self.DRVb,))
                for nm_s, nm_d in ((("abr", j, d), ("hbr", j, d)), (("abi", j, d), ("hbi", j, d))):
                    self.V(lambda e, nm_s=nm_s, nm_d=nm_d: e.tensor_scalar(out=self.drv(nm_d, 0, 8), in0=self.prm(nm_s, 0, 8),
                                                                           scalar1=0.5, scalar2=None, op0=ALU.mult),
                           (self.PRMb, self.DRVb), (self.DRVb,))
        self.units.put((u, ub))

    def ffn(self, l, tiles, s):
        fwu = self.din("fwu%d" % l, [2 * NJ, 128, 1024], F32R)
        fwd = self.din("fwd%d" % l, [8, 128, NJ * 128], F32R)
        for ti, t in enumerate(tiles):
            srcs = [t["x"](c) for c in range(8)]
            self.norm_mod(srcs, ti * 8, self.drv(("gsf", l, s), 0, 8), self.drv(("shf", l, s), 0, 8), ti)
        for (j0, j1) in FFN_GROUPS:
            hm = {}
            for j in range(j0, j1):
                sg, sgb = self.wload(fwu[j], 1024)
                sv, svb = self.wload(fwu[NJ + j], 1024)
                for ti, t in enumerate(tiles):
                    n = t["n"]
                    pg, pgb = self.psums.get()
                    pv, pvb = self.psums.get()
                    hl = [R(self.HLt[:, ti * 8 + k, 0:n]) for k in range(8)]
                    hlb = [self.HLb[ti * 8 + k] for k in range(8)]
                    self.mm(pg, pgb, [(sg[:, k * 128:(k + 1) * 128], hl[k]) for k in range(8)], [sgb] + hlb, n)
                    self.mm(pv, pvb, [(sv[:, k * 128:(k + 1) * 128], hl[k]) for k in range(8)], [svb] + hlb, n)
                    acc, accb = self.units.get()
                    th, thb = self.units.get()
                    h, hb = self.runits.get()
                    hm[(j, ti)] = (h, hb)
                    w0 = self.prm(("fcw", l, 0), j)
                    w1 = self.prm(("fcw", l, 1), j)
                    w2 = self.prm(("fcw", l, 2), j)
                    bb = self.prm(("fcb", l), j)
                    self.A(lambda e, acc=acc, pg=pg, n=n, w1=w1, bb=bb: e.activation(out=acc[:, 0:n], in_=pg[:, 0:n], func=AF.Identity,
                                                                                  scale=w1, bias=bb), (pgb, self.PRMb), (accb,))
                    rl = t["rowlen"]
                    a3 = acc[:, 0:n].rearrange("p (r w) -> p r w", w=rl)
                    g3 = pg[:, 0:n].rearrange("p (r w) -> p r w", w=rl)
                    self.V(lambda e, a3=a3, g3=g3, w0=w0, rl=rl: e.scalar_tensor_tensor(
                        out=a3[:, :, 1:rl], in0=g3[:, :, 0:rl - 1], scalar=w0, in1=a3[:, :, 1:rl], op0=ALU.mult, op1=ALU.add),
                        (pgb, accb, self.PRMb), (accb,))
                    self.V(lambda e, a3=a3, g3=g3, w2=w2, rl=rl: e.scalar_tensor_tensor(
                        out=a3[:, :, 0:rl - 1], in0=g3[:, :, 1:rl], scalar=w2, in1=a3[:, :, 0:rl - 1], op0=ALU.mult, op1=ALU.add),
                        (pgb, accb, self.PRMb), (accb,))
                    self.A(lambda e, th=th, acc=acc, n=n: e.activation(out=th[:, 0:n], in_=acc[:, 0:n], func=AF.Tanh, scale=0.5),
                           (accb,), (thb,))
                    self.V(lambda e, th=th, acc=acc, n=n: e.scalar_tensor_tensor(
                        out=th[:, 0:n], in0=th[:, 0:n], scalar=1.0, in1=acc[:, 0:n], op0=ALU.add, op1=ALU.mult),
                        (thb, accb), (thb,))
                    self.V(lambda e, h=h, th=th, pv=pv, n=n: e.scalar_tensor_tensor(
                        out=R(h[:, 0:n]), in0=th[:, 0:n], scalar=0.5, in1=pv[:, 0:n], op0=ALU.mult, op1=ALU.mult),
                        (thb, pvb), (hb,))
                    self.psums.put((pg, pgb))
                    self.psums.put((pv, pvb))
                    self.units.put((acc, accb))
                    self.units.put((th, thb))
                self.slots.put((sg, sgb))
                self.slots.put((sv, svb))
            ng = j1 - j0
            for m in range(8):
                sd, sdb = self.wload(fwd[m][:, j0 * 128:j1 * 128], ng * 128)
                for ti, t in enumerate(tiles):
                    n = t["n"]
                    ps, psb = self.psums.get()
                    pieces = [(sd[:, (j - j0) * 128:(j - j0 + 1) * 128], R(hm[(j, ti)][0][:, 0:n])) for j in range(j0, j1)]
                    self.mm(ps, psb, pieces, [sdb] + [hm[(j, ti)][1] for j in range(j0, j1)], n)
                    xap, xb = t["x"](m)
                    gf = self.drv(("gf", l, s), m)
                    self.V(lambda e, xap=xap, ps=ps, n=n, gf=gf: e.scalar_tensor_tensor(
                        out=xap, in0=ps[:, 0:n], scalar=gf, in1=xap, op0=ALU.mult, op1=ALU.add),
                        [psb, self.DRVb] + list(xb), xb)
                    self.psums.put((ps, psb))
                self.slots.put((sd, sdb))
            for v in hm.values():
                self.runits.put(v)

    def lat_tile(self, r0, r1):
        a, b = col(r0), col(r1)
        return {"x": (lambda c, a=a, b=b: (self.X[:, c, a:b], self.xbufs(c, a, b))), "n": b - a, "rowlen": ROWW,
                "a": a, "b": b}

    def ctx_tile(self):
        return {"x": (lambda c: (self.HC[:, c, :], [self.HCb[c]])), "n": CTXN, "rowlen": CTXN, "a": 0, "b": CTXN}

    def ffn_layer(self, l, with_ctx):
        if with_ctx:
            self.ffn(l, [self.ctx_tile()], 1)
        lo, hi = layer_rows(l)
        tl = split_rows(lo, hi, 7)
        i = 0
        while i < len(tl):
            grp = tl[i:i + 2]
            self.ffn(l, [self.lat_tile(r0, r1) for (r0, r1) in grp], 0)
            i += 2

    def sc_layer(self, l, with_ctx):
        j = l // 2
        bwi = self.din("bwi%d" % j, [24, 128, 1024], F32R)
        bwo = self.din("bwo%d" % j, [8, 128, 1024], F32R)
        lo, hi = layer_rows(l)
        tl = split_rows(lo, hi, 6)
        tiles = []
        if with_ctx:
            tiles.append(dict(ctx=True, a=0, b=CTXN, ha=0, hb=CTXN, masked=False))
        for (r0, r1) in tl:
            tiles.append(dict(ctx=False, a=col(r0), b=col(r1), ha=col(r0 - 1), hb=col(r1 + 1),
                              masked=(r0 == lo or r1 == hi)))

        def hl_stage(t, hbuf):
            if t["ctx"]:
                srcs = [(self.HC[:, c, :], [self.HCb[c]]) for c in range(8)]
                s = 1
            else:
                srcs = [(self.X[:, c, t["ha"]:t["hb"]], self.xbufs(c, t["ha"], t["hb"])) for c in range(8)]
                s = 0
            self.norm_mod(srcs, hbuf * 8, self.drv(("gsm", l, s), 0, 8), self.drv(("shm", l, s), 0, 8), hbuf)

        hl_stage(tiles[0], 0)
        for ti, t in enumerate(tiles):
            hbuf = ti % 2
            if ti + 1 < len(tiles):
                hl_stage(tiles[ti + 1], 1 - hbuf)
            s = 1 if t["ctx"] else 0
            nh = t["hb"] - t["ha"]
            n = t["b"] - t["a"]
            off = t["a"] - t["ha"]
            hl = [R(self.HLt[:, hbuf * 8 + k, 0:nh]) for k in range(8)]
            hlc = [R(self.HLt[:, hbuf * 8 + k, off:off + n]) for k in range(8)]
            hlb = [self.HLb[hbuf * 8 + k] for k in range(8)]
            mins = []
            for c in range(8):
                sB, sBb = self.wload(bwi[c], 1024)
                sC, sCb = self.wload(bwi[8 + c], 1024)
                sV, sVb = self.wload(bwi[16 + c], 1024)
                pB, pBb = self.psums.get()
                pC, pCb = self.psums.get()
                pV, pVb = self.psums.get()
                self.mm(pB, pBb, [(sB[:, k * 128:(k + 1) * 128], hlc[k]) for k in range(8)], [sBb] + hlb, n)
                self.mm(pC, pCb, [(sC[:, k * 128:(k + 1) * 128], hl[k]) for k in range(8)], [sCb] + hlb, nh)
                self.mm(pV, pVb, [(sV[:, k * 128:(k + 1) * 128], hl[k]) for k in range(8)], [sVb] + hlb, nh)
                for sl in ((sB, sBb), (sC, sCb), (sV, sVb)):
                    self.slots.put(sl)
                gc, gcb = self.units.get()
                cv, cvb = self.units.get()
                acc, accb = self.units.get()
                mn, mnb = self.runits.get()
                mins.append((mn, mnb))
                self.A(lambda e, gc=gc, pC=pC, nh=nh: e.activation(out=gc[:, 0:nh], in_=pC[:, 0:nh], func=AF.Copy), (pCb,), (gcb,))
                self.V(lambda e, cv=cv, gc=gc, pV=pV, nh=nh: e.tensor_tensor(out=cv[:, 0:nh], in0=gc[:, 0:nh], in1=pV[:, 0:nh], op=ALU.mult),
                       (gcb, pVb), (cvb,))
                if t["masked"]:
                    vm = self.vm(t["ha"], t["hb"])
                    self.V(lambda e, cv=cv, vm=vm, nh=nh: e.tensor_tensor(out=cv[:, 0:nh], in0=cv[:, 0:nh], in1=vm, op=ALU.mult),
                           (cvb, self.VMb), (cvb,))
                w0 = self.prm(("bcw", j, 0), c)
                w1 = self.prm(("bcw", j, 1), c)
                w2 = self.prm(("bcw", j, 2), c)
                bb = self.prm(("bcb", j), c)
                self.A(lambda e, acc=acc, cv=cv, off=off, n=n, w1=w1, bb=bb: e.activation(
                    out=acc[:, 0:n], in_=cv[:, off:off + n], func=AF.Identity, scale=w1, bias=bb), (cvb, self.PRMb), (accb,))
                if t["ctx"]:
                    self.V(lambda e, acc=acc, cv=cv, n=n, w0=w0: e.scalar_tensor_tensor(
                        out=acc[:, 1:n], in0=cv[:, 0:n - 1], scalar=w0, in1=acc[:, 1:n], op0=ALU.mult, op1=ALU.add),
                        (cvb, accb, self.PRMb), (accb,))
                    self.V(lambda e, acc=acc, cv=cv, n=n, w2=w2: e.scalar_tensor_tensor(
                        out=acc[:, 0:n - 1], in0=cv[:, 1:n], scalar=w2, in1=acc[:, 0:n - 1], op0=ALU.mult, op1=ALU.add),
                        (cvb, accb, self.PRMb), (accb,))
                else:
                    self.V(lambda e, acc=acc, cv=cv, n=n, w0=w0: e.scalar_tensor_tensor(
                        out=acc[:, 0:n], in0=cv[:, 0:n], scalar=w0, in1=acc[:, 0:n], op0=ALU.mult, op1=ALU.add),
                        (cvb, accb, self.PRMb), (accb,))
                    self.V(lambda e, acc=acc, cv=cv, n=n, w2=w2: e.scalar_tensor_tensor(
                        out=acc[:, 0:n], in0=cv[:, 128:128 + n], scalar=w2, in1=acc[:, 0:n], op0=ALU.mult, op1=ALU.add),
                        (cvb, accb, self.PRMb), (accb,))
                self.V(lambda e, mn=mn, acc=acc, pB=pB, n=n: e.tensor_tensor(out=R(mn[:, 0:n]), in0=acc[:, 0:n], in1=pB[:, 0:n], op=ALU.mult),
                       (accb, pBb), (mnb,))
                for pp in ((pB, pBb), (pC, pCb), (pV, pVb)):
                    self.psums.put(pp)
                for uu in ((gc, gcb), (cv, cvb), (acc, accb)):
                    self.units.put(uu)
            self.out_proj(bwo, mins, t, n, ("gm", l, s))
            for uu in mins:
                self.runits.put(uu)

    def out_proj(self, wdram, mins, t, n, gname):
        for m in range(8):
            so, sob = self.wload(wdram[m], 1024)
            ps, psb = self.psums.get()
            self.mm(ps, psb, [(so[:, k * 128:(k + 1) * 128], R(mins[k][0][:, 0:n])) for k in range(8)],
                    [sob] + [mins[k][1] for k in range(8)], n)
            if t["ctx"]:
                xap, xb = self.HC[:, m, :], [self.HCb[m]]
            else:
                xap, xb = self.X[:, m, t["a"]:t["b"]], self.xbufs(m, t["a"], t["b"])
            g = self.drv(gname, m)
            self.V(lambda e, xap=xap, ps=ps, n=n, g=g: e.scalar_tensor_tensor(
                out=xap, in0=ps[:, 0:n], scalar=g, in1=xap, op0=ALU.mult, op1=ALU.add),
                [psb, self.DRVb] + list(xb), xb)
            self.psums.put((ps, psb))
            self.slots.put((so, sob))

    def rg_plan(self, l):
        lo, hi = layer_rows(l)
        tl = split_rows(lo, hi, 7)
        capf, capb = col(lo + OWN // ROWW), col(hi - OWN // ROWW)
        tiles = []
        segi = 0
        for (r0, r1) in tl:
            a, b = col(r0), col(r1)
            cuts = sorted(set([a, b] + [c for c in (capf, capb) if a < c < b]))
            segs = []
            for s0, s1 in zip(cuts[:-1], cuts[1:]):
                segs.append(dict(a=s0, b=s1, idx=segi, fwd=(s1 <= capf), pubb=(s0 >= capb)))
                segi += 1
            tiles.append(dict(ctx=False, a=a, b=b, segs=segs, masked=(r0 == lo or r1 == hi)))
        assert segi <= 8
        return tiles

    def rg_layer(self, l, phase, with_ctx=False, ctx_out=False, latent=True):
        j = l // 2
        awx = self.din("awx%d" % j, [8, 128, 1024], F32R)
        awg = self.din("awg%d" % j, [8, 128, 1024], F32R)
        awo = self.din("awo%d" % j, [8, 128, 1024], F32R)
        agt = self.din("agt%d" % j, [2, 2, 128, 2048], F32R)
        tiles = []
        if with_ctx:
            tiles.append(dict(ctx=True, a=0, b=CTXN, segs=[dict(a=0, b=CTXN, idx=0, fwd=True, pubb=False)], masked=False,
                              ctx_out=ctx_out))
        if latent:
            tiles += self.rg_plan(l)

        def hl_stage(t, hbuf):
            if t["ctx"]:
                srcs = [(self.HC[:, c, :], [self.HCb[c]]) for c in range(8)]
                s = 1
            else:
                srcs = [(self.X[:, c, t["a"] - 2:t["b"] + 2], self.xbufs(c, t["a"] - 2, t["b"] + 2)) for c in range(8)]
                s = 0
            self.norm_mod(srcs, hbuf * 8, self.drv(("gsm", l, s), 0, 8), self.drv(("shm", l, s), 0, 8), hbuf)

        if phase == 1:
            self.V(lambda e: e.memset(self.AGG[:, :], 0.0), (self.AGGb,), (self.AGGb,))
            self.V(lambda e: e.memset(self.CAR[:, 0:8], 0.0), (self.CARb,), (self.CARb,))
            self.V(lambda e: e.memset(self.SM[:, 0:64], 0.0), (self.SMb,), (self.SMb,))
        hl_stage(tiles[0], 0)
        ltile = 0
        for ti, t in enumerate(tiles):
            hbuf = ti % 2
            if ti + 1 < len(tiles):
                hl_stage(tiles[ti + 1], 1 - hbuf)
            self.rg_tile(l, j, t, phase, hbuf, awx, awg, awo, agt, ltile)
            if not t["ctx"]:
                ltile += 1
        if phase == 1:
            self.rg_publish(l, j, tiles)

    def rg_tile(self, l, j, t, phase, hbuf, awx, awg, awo, agt, ltile):
        ctx = t["ctx"]
        s = 1 if ctx else 0
        n = t["b"] - t["a"]
        na = n if ctx else n + 4
        hl = [R(self.HLt[:, hbuf * 8 + k, 0:na]) for k in range(8)]
        hlc = [R(self.HLt[:, hbuf * 8 + k, (0 if ctx else 2):(0 if ctx else 2) + n]) for k in range(8)]
        hlb = [self.HLb[hbuf * 8 + k] for k in range(8)]
        full = (phase == 2) and (not ctx or t.get("ctx_out", False))
        mins = []
        for h in range(4):
            U = {}
            gslots = {}
            for m in (2 * h, 2 * h + 1):
                sx, sxb = self.wload(awx[m], 1024)
                ps, psb = self.psums.get()
                self.mm(ps, psb, [(sx[:, k * 128:(k + 1) * 128], hl[k]) for k in range(8)], [sxb] + hlb, na)
                self.slots.put((sx, sxb))
                up, upb = self.units.get()
                if ctx:
                    self.V(lambda e, up=up: e.memset(up[:, 0:2], 0.0), (upb,), (upb,))
                    self.V(lambda e, up=up, n=n: e.memset(up[:, n + 2:n + 4], 0.0), (upb,), (upb,))
                    self.A(lambda e, up=up, ps=ps, n=n: e.activation(out=up[:, 2:2 + n], in_=ps[:, 0:n], func=AF.Copy), (psb, upb), (upb,))
                elif t["masked"]:
                    vm = self.vm(t["a"] - 2, t["b"] + 2)
                    self.V(lambda e, up=up, ps=ps, vm=vm, na=na: e.tensor_tensor(out=up[:, 0:na], in0=ps[:, 0:na], in1=vm, op=ALU.mult),
                           (psb, self.VMb), (upb,))
                else:
                    self.A(lambda e, up=up, ps=ps, na=na: e.activation(out=up[:, 0:na], in_=ps[:, 0:na], func=AF.Copy), (psb,), (upb,))
                self.psums.put((ps, psb))
                u, ub = self.runits.get()
                cw = [self.prm(("acw", j, tp), m) for tp in range(4)]
                cb = self.prm(("acb", j), m)
                self.A(lambda e, u=u, up=up, n=n, cw=cw, cb=cb: e.activation(out=R(u[:, 0:n]), in_=up[:, 2:2 + n], func=AF.Identity,
                                                                          scale=cw[2], bias=cb), (upb, self.PRMb), (ub,))
                for tp, o in ((0, 0), (1, 1), (3, 3)):
                    self.V(lambda e, u=u, up=up, n=n, w=cw[tp], o=o: e.scalar_tensor_tensor(
                        out=R(u[:, 0:n]), in0=up[:, o:o + n], scalar=w, in1=u[:, 0:n], op0=ALU.mult, op1=ALU.add),
                        (upb, ub, self.PRMb), (ub,))
                self.units.put((up, upb))
                U[m] = (u, ub)
            pairs = []
            for d in range(2):
                for gi in range(2):
                    src = agt[d, gi][:, (h // 2) * 1024:(h // 2) * 1024 + 1024]
                    gslots[(d, gi)] = self.wload(src, 1024)
            for m in (2 * h, 2 * h + 1):
                ml = m % 2
                for d in range(2):
                    base = (h % 2) * 512 + ml * 256
                    pr, prb = self.psums.get()
                    pi, pib = self.psums.get()
                    sr, srb = gslots[(d, 0)]
                    si, sib = gslots[(d, 1)]
                    ur = [R(U[2 * h + k][0][:, 0:n]) for k in range(2)]
                    urb = [U[2 * h + k][1] for k in range(2)]
                    self.mm(pr, prb, [(sr[:, base + k * 128:base + (k + 1) * 128], ur[k]) for k in range(2)], [srb] + urb, n)
                    self.mm(pi, pib, [(si[:, base + k * 128:base + (k + 1) * 128], ur[k]) for k in range(2)], [sib] + urb, n)
                    A_, Ab = self.units.get()
                    P_, Pb = self.units.get()
                    Q_, Qb = self.units.get()
                    hbr = self.drv(("hbr", j, d), m)
                    hbi = self.drv(("hbi", j, d), m)
                    hcl = self.drv(("hcl", j, d), m)
                    self.A(lambda e, Q_=Q_, pr=pr, n=n, hbr=hbr: e.activation(out=Q_[:, 0:n], in_=pr[:, 0:n], func=AF.Tanh, scale=0.5, bias=hbr),
                           (prb, self.DRVb), (Qb,))
                    if phase == 1:
                        for sg in t["segs"]:
                            o0, o1 = sg["a"] - t["a"], sg["b"] - t["a"]
                            acol = self.SM[:, 64 + sg["idx"] * 16 + d * 8 + m:64 + sg["idx"] * 16 + d * 8 + m + 1]
                            if t["masked"]:
                                vm = self.vm(sg["a"], sg["b"])
                                self.V(lambda e, Q_=Q_, o0=o0, o1=o1, vm=vm, acol=acol: e.scalar_tensor_tensor(
                                    out=Q_[:, o0:o1], in0=Q_[:, o0:o1], scalar=1.0, in1=vm, op0=ALU.add, op1=ALU.mult, accum_out=acol),
                                    (Qb, self.VMb, self.SMb), (Qb, self.SMb))
                            else:
                                self.V(lambda e, Q_=Q_, o0=o0, o1=o1, acol=acol: e.tensor_scalar(
                                    out=Q_[:, o0:o1], in0=Q_[:, o0:o1], scalar1=1.0, scalar2=None, op0=ALU.add, op1=ALU.add, accum_out=acol),
                                    (Qb, self.SMb), (Qb, self.SMb))
                        self.A(lambda e, A_=A_, Q_=Q_, n=n, hcl=hcl: e.activation(out=A_[:, 0:n], in_=Q_[:, 0:n], func=AF.Exp, scale=hcl),
                               (Qb, self.DRVb), (Ab,))
                    elif t["masked"]:
                        vm = self.vm(t["a"], t["b"])
                        self.V(lambda e, Q_=Q_, n=n, vm=vm: e.scalar_tensor_tensor(
                            out=Q_[:, 0:n], in0=Q_[:, 0:n], scalar=1.0, in1=vm, op0=ALU.add, op1=ALU.mult), (Qb, self.VMb), (Qb,))
                        self.A(lambda e, A_=A_, Q_=Q_, n=n, hcl=hcl: e.activation(out=A_[:, 0:n], in_=Q_[:, 0:n], func=AF.Exp, scale=hcl),
                               (Qb, self.DRVb), (Ab,))
                    else:
                        self.A(lambda e, A_=A_, Q_=Q_, n=n, hcl=hcl: e.activation(out=A_[:, 0:n], in_=Q_[:, 0:n], func=AF.Exp, scale=hcl, bias=hcl),
                               (Qb, self.DRVb), (Ab,))
                    self.A(lambda e, P_=P_, pi=pi, n=n, hbi=hbi: e.activation(out=P_[:, 0:n], in_=pi[:, 0:n], func=AF.Tanh, scale=0.5, bias=hbi),
                           (pib, self.DRVb), (Pb,))
                    self.psums.put((pr, prb))
                    self.psums.put((pi, pib))
                    uu, uub = U[m]
                    self.V(lambda e, P_=P_, uu=uu, n=n: e.scalar_tensor_tensor(
                        out=P_[:, 0:n], in0=P_[:, 0:n], scalar=1.0, in1=uu[:, 0:n], op0=ALU.add, op1=ALU.mult), (Pb, uub), (Pb,))
                    if t["masked"]:
                        vmt = self.vm(t["a"], t["b"])
                        self.V(lambda e, A_=A_, n=n, vmt=vmt: e.scalar_tensor_tensor(
                            out=A_[:, 0:n], in0=A_[:, 0:n], scalar=1.0, in1=vmt, op0=ALU.subtract, op1=ALU.mult), (Ab, self.VMb), (Ab,))
                        self.V(lambda e, A_=A_, n=n: e.tensor_scalar(out=A_[:, 0:n], in0=A_[:, 0:n], scalar1=1.0, scalar2=None, op0=ALU.add),
                               (Ab,), (Ab,))
                        self.V(lambda e, P_=P_, n=n, vmt=vmt: e.tensor_tensor(out=P_[:, 0:n], in0=P_[:, 0:n], in1=vmt, op=ALU.mult),
                               (Pb, self.VMb), (Pb,))
                    self.A(lambda e, Q_=Q_, A_=A_, n=n: e.activation(out=Q_[:, 0:n], in_=A_[:, 0:n], func=AF.Square), (Ab, Qb), (Qb,))
                    self.A(lambda e, Q_=Q_, n=n: e.activation(out=Q_[:, 0:n], in_=Q_[:, 0:n], func=AF.Relu, scale=-1.0, bias=1.0), (Qb,), (Qb,))
                    pairs.append((m, d, A_, Ab, P_, Pb, Q_, Qb))
            for (d, gi), sl in gslots.items():
                self.slots.put(sl)
            for (m, d, A_, Ab, P_, Pb, Q_, Qb) in pairs:
                self.A(lambda e, Q_=Q_, n=n: e.activation(out=Q_[:, 0:n], in_=Q_[:, 0:n], func=AF.Sqrt), (Qb,), (Qb,))
            hsum = {}
            for (m, d, A_, Ab, P_, Pb, Q_, Qb) in pairs:
                self.V(lambda e, P_=P_, Q_=Q_, n=n: e.scalar_tensor_tensor(
                    out=P_[:, 0:n], in0=P_[:, 0:n], scalar=0.5, in1=Q_[:, 0:n], op0=ALU.mult, op1=ALU.mult), (Pb, Qb), (Pb,))
                if phase == 1:
                    for sg in t["segs"]:
                        o0, o1 = sg["a"] - t["a"], sg["b"] - t["a"]
                        if d == 0:
                            if not sg["fwd"]:
                                continue
                            init = self.CAR[:, m:m + 1]
                            self.V(lambda e, Q_=Q_, A_=A_, P_=P_, o0=o0, o1=o1, init=init: e.tensor_tensor_scan(
                                out=Q_[:, o0:o1], data0=A_[:, o0:o1], data1=P_[:, o0:o1], initial=init, op0=ALU.mult, op1=ALU.add),
                                (Ab, Pb, self.CARb, Qb), (Qb,))
                            self.V(lambda e, Q_=Q_, o1=o1, init=init: e.tensor_copy(out=init, in_=Q_[:, o1 - 1:o1]), (Qb, self.CARb), (self.CARb,))
                        else:
                            self.V(lambda e, Q_=Q_, A_=A_, P_=P_, o0=o0, o1=o1: e.tensor_tensor_scan(
                                out=Q_[:, o0:o1][:, ::-1], data0=A_[:, o0:o1][:, ::-1], data1=P_[:, o0:o1][:, ::-1], initial=0.0,
                                op0=ALU.mult, op1=ALU.add), (Ab, Pb, Qb), (Qb,))
                            bcol = self.AGG[:, 48 + sg["idx"] * 16 + 8 + m:48 + sg["idx"] * 16 + 8 + m + 1]
                            self.V(lambda e, Q_=Q_, o0=o0, bcol=bcol: e.tensor_copy(out=bcol, in_=Q_[:, o0:o0 + 1]), (Qb, self.AGGb), (self.AGGb,))
                else:
                    if d == 0:
                        if ctx:
                            init = 0.0
                            rd = ()
                        else:
                            init = self.CAR[:, m:m + 1]
                            rd = (self.CARb,)
                        self.V(lambda e, Q_=Q_, A_=A_, P_=P_, n=n, init=init: e.tensor_tensor_scan(
                            out=Q_[:, 0:n], data0=A_[:, 0:n], data1=P_[:, 0:n], initial=init, op0=ALU.mult, op1=ALU.add),
                            (Ab, Pb, Qb) + rd, (Qb,))
                        if ctx:
                            self.V(lambda e, Q_=Q_, n=n, m=m: e.tensor_copy(out=self.HCS[:, m:m + 1], in_=Q_[:, n - 1:n]), (Qb, self.HCSb), (self.HCSb,))
                        else:
                            self.V(lambda e, Q_=Q_, n=n, init=init: e.tensor_copy(out=init, in_=Q_[:, n - 1:n]), (Qb, self.CARb), (self.CARb,))
                    else:
                        if ctx:
                            init = 0.0
                            rd = ()
                        else:
                            init = self.CAR[:, 8 + ltile * 8 + m:8 + ltile * 8 + m + 1]
                            rd = (self.CARb,)
                        self.V(lambda e, Q_=Q_, A_=A_, P_=P_, n=n, init=init: e.tensor_tensor_scan(
                            out=Q_[:, 0:n][:, ::-1], data0=A_[:, 0:n][:, ::-1], data1=P_[:, 0:n][:, ::-1], initial=init,
                            op0=ALU.mult, op1=ALU.add), (Ab, Pb, Qb) + rd, (Qb,))
                        if ctx:
                            self.V(lambda e, Q_=Q_, m=m: e.tensor_copy(out=self.HCS[:, 8 + m:9 + m], in_=Q_[:, 0:1]), (Qb, self.HCSb), (self.HCSb,))
                self.units.put((A_, Ab))
                self.units.put((P_, Pb))
                if full:
                    if d == 0:
                        hsum[m] = (Q_, Qb)
                    else:
                        hq, hqb = hsum[m]
                        self.V(lambda e, hq=hq, Q_=Q_, n=n: e.tensor_tensor(out=hq[:, 0:n], in0=hq[:, 0:n], in1=Q_[:, 0:n], op=ALU.add), (hqb, Qb), (hqb,))
                        self.units.put((Q_, Qb))
                else:
                    self.units.put((Q_, Qb))
            for m in (2 * h, 2 * h + 1):
                self.runits.put(U[m])
            if full:
                for m in (2 * h, 2 * h + 1):
                    sgw, sgwb = self.wload(awg[m], 1024)
                    pz, pzb = self.psums.get()
                    self.mm(pz, pzb, [(sgw[:, k * 128:(k + 1) * 128], hlc[k]) for k in range(8)], [sgwb] + hlb, n)
                    self.slots.put((sgw, sgwb))
                    z2, z2b = self.units.get()
                    self.A(lambda e, z2=z2, pz=pz, n=n: e.activation(out=z2[:, 0:n], in_=pz[:, 0:n], func=AF.Square), (pzb,), (z2b,))
                    self.V(lambda e, z2=z2, n=n: e.tensor_scalar(out=z2[:, 0:n], in0=z2[:, 0:n], scalar1=GELU_K * GELU_C, scalar2=GELU_K,
                                                                  op0=ALU.mult, op1=ALU.add), (z2b,), (z2b,))
                    self.V(lambda e, z2=z2, pz=pz, n=n: e.tensor_tensor(out=z2[:, 0:n], in0=z2[:, 0:n], in1=pz[:, 0:n], op=ALU.mult), (z2b, pzb), (z2b,))
                    self.A(lambda e, z2=z2, n=n: e.activation(out=z2[:, 0:n], in_=z2[:, 0:n], func=AF.Tanh), (z2b,), (z2b,))
                    self.V(lambda e, z2=z2, pz=pz, n=n: e.scalar_tensor_tensor(
                        out=z2[:, 0:n], in0=z2[:, 0:n], scalar=1.0, in1=pz[:, 0:n], op0=ALU.add, op1=ALU.mult), (z2b, pzb), (z2b,))
                    self.psums.put((pz, pzb))
                    hq, hqb = hsum[m]
                    mn, mnb = self.runits.get()
                    self.V(lambda e, hq=hq, z2=z2, n=n, mn=mn: e.scalar_tensor_tensor(
                        out=R(mn[:, 0:n]), in0=z2[:, 0:n], scalar=0.5, in1=hq[:, 0:n], op0=ALU.mult, op1=ALU.mult), (z2b, hqb), (mnb,))
                    self.units.put((z2, z2b))
                    self.units.put((hq, hqb))
                    mins.append((mn, mnb))
        if full:
            self.out_proj(awo, mins, t, n, ("gm", l, s))
            for uu in mins:
                self.runits.put(uu)

    def rg_publish(self, l, j, tiles):
        segs = [sg for t in tiles for sg in t["segs"]]
        for sg in segs:
            i = sg["idx"]
            src = self.SM[:, 64 + i * 16 + 8:64 + i * 16 + 16]
            dst = self.AGG[:, 48 + i * 16:48 + i * 16 + 8]
            self.V(lambda e, src=src, dst=dst: e.tensor_tensor(out=dst, in0=src, in1=self.drv(("hcl", j, 1), 0, 8), op=ALU.mult),
                   (self.SMb, self.DRVb, self.AGGb), (self.AGGb,))
            self.A(lambda e, dst=dst: e.activation(out=dst, in_=dst, func=AF.Exp), (self.AGGb,), (self.AGGb,))
        first = True
        for sg in segs:
            if not sg["fwd"]:
                continue
            i = sg["idx"]
            src = self.SM[:, 64 + i * 16:64 + i * 16 + 8]
            if first:
                self.V(lambda e, src=src: e.tensor_copy(out=self.SM[:, 0:8], in_=src), (self.SMb,), (self.SMb,))
                first = False
            else:
                self.V(lambda e, src=src: e.tensor_tensor(out=self.SM[:, 0:8], in0=self.SM[:, 0:8], in1=src, op=ALU.add), (self.SMb,), (self.SMb,))
        self.V(lambda e: e.tensor_tensor(out=self.AGG[:, 0:8], in0=self.SM[:, 0:8], in1=self.drv(("hcl", j, 0), 0, 8), op=ALU.mult),
               (self.SMb, self.DRVb, self.AGGb), (self.AGGb,))
        self.A(lambda e: e.activation(out=self.AGG[:, 0:8], in_=self.AGG[:, 0:8], func=AF.Exp), (self.AGGb,), (self.AGGb,))
        self.V(lambda e: e.tensor_copy(out=self.AGG[:, 8:16], in_=self.CAR[:, 0:8]), (self.CARb, self.AGGb), (self.AGGb,))
        self.V(lambda e: e.memset(self.AGG[:, 16:24], 1.0), (self.AGGb,), (self.AGGb,))
        self.V(lambda e: e.memset(self.AGG[:, 24:32], 0.0), (self.AGGb,), (self.AGGb,))
        for sg in reversed(segs):
            if not sg["pubb"]:
                continue
            i = sg["idx"]
            As = self.AGG[:, 48 + i * 16:48 + i * 16 + 8]
            Bs = self.AGG[:, 48 + i * 16 + 8:48 + i * 16 + 16]
            self.V(lambda e, As=As: e.tensor_tensor(out=self.AGG[:, 16:24], in0=self.AGG[:, 16:24], in1=As, op=ALU.mult), (self.AGGb,), (self.AGGb,))
            self.V(lambda e, As=As: e.tensor_tensor(out=self.AGG[:, 24:32], in0=self.AGG[:, 24:32], in1=As, op=ALU.mult), (self.AGGb,), (self.AGGb,))
            self.V(lambda e, Bs=Bs: e.tensor_tensor(out=self.AGG[:, 24:32], in0=self.AGG[:, 24:32], in1=Bs, op=ALU.add), (self.AGGb,), (self.AGGb,))

    def rg_carry(self, l):
        tiles = self.rg_plan(l)
        G = self.GATH
        tmp = self.SM[:, 16:24]
        self.V(lambda e: e.tensor_copy(out=self.CAR[:, 0:8], in_=self.HCS[:, 0:8]), (self.HCSb, self.CARb), (self.CARb,))
        cur = self.CAR[:, 0:8]
        for jj in range(NCORE):
            Af, Bf = G[:, jj * 32:jj * 32 + 8], G[:, jj * 32 + 8:jj * 32 + 16]
            mk = self.prm("mf", jj)
            self.step_sel(cur, Af, Bf, mk, tmp)
        curb = self.SM[:, 24:32]
        self.V(lambda e: e.tensor_copy(out=curb, in_=self.HCS[:, 8:16]), (self.HCSb, self.SMb), (self.SMb,))
        for jj in reversed(range(NCORE)):
            Ab, Bb = G[:, jj * 32 + 16:jj * 32 + 24], G[:, jj * 32 + 24:jj * 32 + 32]
            mk = self.prm("mb", jj)
            self.step_sel(curb, Ab, Bb, mk, tmp)
        nt = len(tiles)
        for ti in reversed(range(nt)):
            dst = self.CAR[:, 8 + ti * 8:16 + ti * 8]
            self.V(lambda e, dst=dst: e.tensor_copy(out=dst, in_=curb), (self.SMb, self.CARb), (self.CARb,))
            for sg in reversed(tiles[ti]["segs"]):
                i = sg["idx"]
                As = self.AGG[:, 48 + i * 16:48 + i * 16 + 8]
                Bs = self.AGG[:, 48 + i * 16 + 8:48 + i * 16 + 16]
                self.V(lambda e, As=As: e.tensor_tensor(out=curb, in0=curb, in1=As, op=ALU.mult), (self.SMb, self.AGGb), (self.SMb,))
                self.V(lambda e, Bs=Bs: e.tensor_tensor(out=curb, in0=curb, in1=Bs, op=ALU.add), (self.SMb, self.AGGb), (self.SMb,))

    def step_sel(self, cur, A_, B_, mk, tmp):
        bufs = (self.SMb, self.CARb, self.GATHb, self.PRMb)
        self.V(lambda e: e.tensor_tensor(out=tmp, in0=cur, in1=A_, op=ALU.mult), bufs, bufs[:2])
        self.V(lambda e: e.tensor_tensor(out=tmp, in0=tmp, in1=B_, op=ALU.add), bufs, bufs[:2])
        self.V(lambda e: e.tensor_tensor(out=tmp, in0=tmp, in1=cur, op=ALU.subtract), bufs, bufs[:2])
        self.V(lambda e: e.scalar_tensor_tensor(out=cur, in0=tmp, scalar=mk, in1=cur, op0=ALU.mult, op1=ALU.add), bufs, bufs[:2])

    def final(self):
        yT = self.dout("yT", [8, 128, OWN])
        lo, hi = HALO_ROWS, NROWS - HALO_ROWS
        for ti, (r0, r1) in enumerate(split_rows(lo, hi, 8)):
            a, b = col(r0), col(r1)
            n = b - a
            ps, psb = self.psums.get()
            sq = [self.runits.get() for _ in range(2)]
            for c in range(8):
                s, sbf = sq[c % 2]
                xa, xb = self.X[:, c, a:b], self.xbufs(c, a, b)
                self.A(lambda e, s=s, xa=xa, n=n: e.activation(out=R(s[:, 0:n]), in_=xa, func=AF.Square), xb, (sbf,))
                self.T(lambda e, ps=ps, s=s, n=n, c=c: e.matmul(ps[:, 0:n], lhsT=R(self.ONES[:, :]), rhs=R(s[:, 0:n]), start=(c == 0), stop=(c == 7)),
                       (sbf, self.ONESb), (psb,))
            for u in sq:
                self.runits.put(u)
            rs, rsb = self.RSTD[:, ti % 2, 0:n], self.RSTDb[ti % 2]
            self.A(lambda e, rs=rs, ps=ps, n=n: e.activation(out=rs, in_=ps[:, 0:n], func=AF.Sqrt, bias=self.epsc, scale=1.0), (psb, self.ONESb), (rsb,))
            self.psums.put((ps, psb))
            self.V(lambda e, rs=rs: e.reciprocal(out=rs, in_=rs), (rsb,), (rsb,))
            for c in range(8):
                o, ob = self.units.get()
                xa, xb = self.X[:, c, a:b], self.xbufs(c, a, b)
                g = self.prm("fing", c)
                self.V(lambda e, o=o, xa=xa, rs=rs, g=g, n=n: e.scalar_tensor_tensor(out=o[:, 0:n], in0=xa, scalar=g, in1=rs, op0=ALU.mult, op1=ALU.mult),
                       list(xb) + [rsb, self.PRMb], (ob,))
                dst = yT[c][:, a - col(lo):b - col(lo)]
                self.out_tok.append(self.DMA(lambda e, dst=dst, o=o, n=n: e.dma_start(out=dst, in_=o[:, 0:n]), (ob,), ()))
                self.units.put((o, ob))

    def build(self):
        nc = self.nc
        self.out_tok = []
        with self.stack:
            self.setup()
            steps = self.steps
            prm_d = self.din("prm", [128, PL.n])
            self.DMA(lambda e: e.dma_start(out=self.PRM[:, :], in_=prm_d), (), (self.PRMb,))
            self.DMA(lambda e: e.dma_start(out=self.VML[:, :], in_=self.din("vml", [128, VMW])), (), (self.VMb,))
            self.DMA(lambda e: e.dma_start(out=self.VMR[:, :], in_=self.din("vmr", [128, VMW])), (), (self.VMb,))
            ou, oub = self.units.get()
            self.G(lambda e: e.memset(ou[:, 0:128], 1.0 / D), (), (oub,))
            self.V(lambda e: e.tensor_copy(out=R(self.ONES[:, :]), in_=ou[:, 0:128]), (oub,), (self.ONESb,))
            self.units.put((ou, oub))
            EPSC = self.sb("EPSC", [128, 1])
            self.epsc = EPSC[:, 0:1]
            self.G(lambda e: e.memset(EPSC[:, :], EPS), (self.ONESb,), (self.ONESb,))
            xsrc = self.din("xin" if "xin" in self.outs["ins"] else "xT", [8, 128, XW])
            for c in range(8):
                self.DMA(lambda e, c=c: e.dma_start(out=self.X[:, c, :], in_=xsrc[c]), (), self.Xb[c])
            if "ctx" in self.outs["ins"]:
                cd = self.din("ctxT", [8, 128, CTXN])
                for c in range(8):
                    self.DMA(lambda e, c=c: e.dma_start(out=self.HC[:, c, :], in_=cd[c]), (), (self.HCb[c],))
            if "drv" in self.outs["ins"]:
                self.DMA(lambda e: e.dma_start(out=self.DRV[:, :], in_=self.din("drv", [128, DL.n])), (), (self.DRVb,))
            if "gath" in self.outs["ins"]:
                self.DMA(lambda e: e.dma_start(out=self.GATH[:, :], in_=self.din("gath", [128, NCORE * 32])), (), (self.GATHb,))
                self.DMA(lambda e: e.dma_start(out=self.AGG[:, :], in_=self.din("aggin", [128, 48 + 128])), (), (self.AGGb,))
            if "hcs" in self.outs["ins"]:
                self.DMA(lambda e: e.dma_start(out=self.HCS[:, :], in_=self.din("hcsin", [128, 16])), (), (self.HCSb,))
            for st in steps:
                getattr(self, st[0])(*st[1:])
            if "drv" in self.outs["outs"]:
                d = self.dout("drv_o", [128, DL.n])
                self.out_tok.append(self.DMA(lambda e: e.dma_start(out=d, in_=self.DRV[:, :]), (self.DRVb,), ()))
            if "agg" in self.outs["outs"]:
                d2 = self.dout("agg_o", [128, 48 + 128])
                self.out_tok.append(self.DMA(lambda e: e.dma_start(out=d2, in_=self.AGG[:, :]), (self.AGGb,), ()))
            if "hcs" in self.outs["outs"]:
                d3 = self.dout("hcs_o", [128, 16])
                self.out_tok.append(self.DMA(lambda e: e.dma_start(out=d3, in_=self.HCS[:, :]), (self.HCSb,), ()))
            if "x" in self.outs["outs"]:
                d4 = self.dout("x_o", [8, 128, XW])
                for c in range(8):
                    self.out_tok.append(self.DMA(lambda e, c=c: e.dma_start(out=d4[c], in_=self.X[:, c, :]), self.Xb[c], ()))
            sp = self.P.engs["sp"]
            fin = {}
            for k, v in self.out_tok:
                fin[k] = max(fin.get(k, 0), v)
            sp.ops.append((list(fin.items()), None, None, 0))
            self.sbuf_left = nc.sbuf_bytes_remaining
            self.emit()
        return nc

    def emit(self):
        nc = self.nc
        sems = self.sems
        P = self.P

        def run(engname, e):
            for waits, fn, tok, inc in P.engs[engname].ops:
                for (k, v) in waits:
                    e.wait_ge(sems[k], v)
                if fn is None:
                    continue
                ins = fn(e)
                ins.then_inc(sems[tok[0]], inc)

        with nc.Block() as block:
            @block.sync
            def _(e):
                run("sp", e)

            @block.scalar
            def _(e):
                run("act", e)

            @block.vector
            def _(e):
                run("dve", e)

            @block.gpsimd
            def _(e):
                run("pool", e)

            @block.tensor
            def _(e):
                run("pe", e)


def launch_specs(depth=DEPTH):
    L = []
    L.append(dict(steps=[("consts",), ("rg_layer", 0, 1)], ins={"xT"}, outs={"drv", "agg"}))
    if depth <= 2:
        st = [("rg_layer", 0, 2, True, True, False), ("rg_carry", 0), ("rg_layer", 0, 2), ("ffn_layer", 0, depth > 1)]
        if depth == 2:
            st += [("sc_layer", 1, False), ("ffn_layer", 1, False)]
        st += [("final",)]
        L.append(dict(steps=st, ins={"xT", "ctx", "drv", "gath"}, outs={"y"}))
        return L
    st = [("rg_layer", 0, 2, True, True, False), ("rg_carry", 0), ("rg_layer", 0, 2), ("ffn_layer", 0, True),
          ("sc_layer", 1, True), ("ffn_layer", 1, True),
          ("rg_layer", 2, 2, True, False, False), ("rg_layer", 2, 1)]
    L.append(dict(steps=st, ins={"xT", "ctx", "drv", "gath"}, outs={"agg", "hcs", "x"}))
    st = [("rg_carry", 2), ("rg_layer", 2, 2), ("ffn_layer", 2, False)]
    if depth == 4:
        st += [("sc_layer", 3, False), ("ffn_layer", 3, False)]
    st += [("final",)]
    L.append(dict(steps=st, ins={"xin", "drv", "gath", "hcs"}, outs={"y"}))
    return L


def run_launch(spec, per_core, shared, extra):
    b = Builder(spec["steps"], dict(ins=spec["ins"], outs=spec["outs"]))
    nc = b.build()
    in_maps = []
    for k in range(NCORE):
        m = {}
        for name in b.dram:
            if name in per_core[k]:
                m[name] = per_core[k][name]
            elif name in shared:
                m[name] = shared[name]
            elif name in extra[k]:
                m[name] = extra[k][name]
        in_maps.append(m)
    declared = set()
    for alloc in nc.allocations:
        if isinstance(alloc, mybir.MemoryLocationSet) and alloc.kind == "ExternalInput":
            declared.add(alloc.memorylocations[0].name)
    in_maps = [{k2: v for k2, v in m.items() if k2 in declared} for m in in_maps]
    res = run_bass_kernel_spmd(nc, in_maps, core_ids=list(range(NCORE)))
    return res.results


def gather_agg(results):
    g = np.concatenate([results[k]["agg_o"][:, 0:32] for k in range(NCORE)], axis=1)
    return np.ascontiguousarray(g)


def kernel(depth=DEPTH, **inp):
    per_core, shared = pack_host(inp)
    specs = launch_specs(depth)
    extra = [dict() for _ in range(NCORE)]
    r1 = run_launch(specs[0], per_core, shared, extra)
    g = gather_agg(r1)
    for k in range(NCORE):
        extra[k] = {"drv": r1[k]["drv_o"], "gath": g, "aggin": r1[k]["agg_o"]}
    r2 = run_launch(specs[1], per_core, shared, extra)
    if len(specs) == 2:
        res = r2
    else:
        g2 = gather_agg(r2)
        for k in range(NCORE):
            extra[k] = {"drv": r1[k]["drv_o"], "gath": g2, "aggin": r2[k]["agg_o"], "hcsin": r2[k]["hcs_o"], "xin": r2[k]["x_o"]}
        res = run_launch(specs[2], per_core, shared, extra)
    out = np.zeros((1, SEQ, D), np.float32)
    for k in range(NCORE):
        y = res[k]["yT"].reshape(D, OWN)
        out[0, k * OWN:(k + 1) * OWN, :] = y.T
    return out
```
